# Optimizing a Trainium2 kernel written in Bass

```python
import jax, jax.numpy as jnp
from jax import lax
import numpy as np

D_MODEL = 1024
BATCH = 8
SEQ = 2048
DEPTH = 4

D_FF = 2816
N_EVEN = (DEPTH + 1) // 2
N_ODD = DEPTH // 2
N_ADA = 9
EPS = 1e-6
NEG_INF = -1e30
Q_BLOCK = 128

CONV_WIDTH = 512
CONV_GROUPS = 8
CONV_TAPS = 3
MLA_HEADS = 8
MLA_NOPE = 64
MLA_ROPE = 32
MLA_V = 64
Q_LORA = 256
KV_LORA = 128
ROPE_THETA = 10000.0
HY_IN = 3 * CONV_WIDTH + Q_LORA + KV_LORA + MLA_ROPE
HY_MIX = CONV_WIDTH + MLA_HEADS * MLA_V
NSA_HEADS = 16
NSA_KV_HEADS = 2
NSA_GROUP = NSA_HEADS // NSA_KV_HEADS
NSA_DK = 64
CMP_BLOCK = 32
CMP_STRIDE = 16
CMP_HID = 128
SLC_BLOCK = 64
N_SEL = 8
WINDOW = 512
FORCE_SCORE = 1e4
NSA_KV_W = 2 * NSA_KV_HEADS * NSA_DK
NSA_IN = NSA_HEADS * NSA_DK + 3 * NSA_KV_W + 3 * NSA_HEADS

kernel_name = "hybrid_conv_mla_nsa_macaron_adaln"


def rms_norm(x, g):
    xf = x.astype(jnp.float32)
    y = xf * lax.rsqrt(jnp.mean(xf * xf, axis=-1, keepdims=True) + EPS)
    return (y * g.astype(jnp.float32)).astype(x.dtype)


def modulate(x, g, shift, scale):
    return rms_norm(x, g) * (1 + scale[:, None]) + shift[:, None]


def swiglu(h, w13, w2):
    a, b = jnp.split(h @ w13, 2, axis=-1)
    return (jax.nn.silu(a) * b) @ w2


def masked_softmax(s, mask):
    s = jnp.where(mask, s.astype(jnp.float32), NEG_INF)
    m = jnp.max(s, axis=-1, keepdims=True)
    p = jnp.exp(s - m) * mask
    return p / jnp.maximum(jnp.sum(p, axis=-1, keepdims=True), 1e-20)


def rope_tables(positions):
    half = MLA_ROPE // 2
    inv = ROPE_THETA ** (-jnp.arange(half, dtype=jnp.float32) / half)
    ang = positions.astype(jnp.float32)[..., None] * inv
    return jnp.cos(ang), jnp.sin(ang)


def apply_rope(t, cos, sin):
    half = t.shape[-1] // 2
    t1 = t[..., :half].astype(jnp.float32)
    t2 = t[..., half:].astype(jnp.float32)
    return jnp.concatenate([t1 * cos - t2 * sin, t1 * sin + t2 * cos], axis=-1).astype(t.dtype)


def to_blocks(t, nqb):
    return jnp.moveaxis(t.reshape(t.shape[0], nqb, Q_BLOCK, *t.shape[2:]), 1, 0)


def from_blocks(o):
    o = jnp.moveaxis(o, 0, 1)
    return o.reshape(o.shape[0], o.shape[1] * o.shape[2], *o.shape[3:])


def short_conv(u, conv_w):
    ch = u.shape[-1]
    return lax.conv_general_dilated(
        u, conv_w[:, None, :], window_strides=(1,), padding=((CONV_TAPS - 1, 0),),
        dimension_numbers=('NWC', 'WIO', 'NWC'), feature_group_count=ch)


def mla_attention(q_nope, q_rope, k_nope, k_rope, v):
    S = q_nope.shape[1]
    nqb = S // Q_BLOCK
    scale = (MLA_NOPE + MLA_ROPE) ** -0.5
    kpos = jnp.arange(S)

    def block(args):
        i, qn, qr = args
        s = (jnp.einsum('bqhd,bkhd->bhqk', qn, k_nope)
             + jnp.einsum('bqhd,bkd->bhqk', qr, k_rope)) * scale
        qpos = i * Q_BLOCK + jnp.arange(Q_BLOCK)
        p = masked_softmax(s, kpos[None, :] <= qpos[:, None])
        return jnp.einsum('bhqk,bkhd->bqhd', p.astype(v.dtype), v)

    o = lax.map(block, (jnp.arange(nqb), to_blocks(q_nope, nqb), to_blocks(q_rope, nqb)))
    return from_blocks(o)


def hybrid_conv_mla(h, cos, sin, w_in, conv_w, q_norm, kv_norm, w_uq, w_ukv, w_out):
    B, S, _ = h.shape
    z = h @ w_in
    cw = CONV_WIDTH
    u, gate_c, gate_b, cq, ckv, kr = jnp.split(
        z, [cw, 2 * cw, 3 * cw, 3 * cw + Q_LORA, 3 * cw + Q_LORA + KV_LORA], axis=-1)
    y_conv = gate_b * short_conv(gate_c * u, conv_w)
    q = (rms_norm(cq, q_norm) @ w_uq).reshape(B, S, MLA_HEADS, MLA_NOPE + MLA_ROPE)
    q_nope, q_rope = q[..., :MLA_NOPE], apply_rope(q[..., MLA_NOPE:], cos[:, :, None], sin[:, :, None])
    kv = (rms_norm(ckv, kv_norm) @ w_ukv).reshape(B, S, MLA_HEADS, MLA_NOPE + MLA_V)
    k_nope, v = kv[..., :MLA_NOPE], kv[..., MLA_NOPE:]
    k_rope = apply_rope(kr, cos, sin)
    y_att = mla_attention(q_nope, q_rope, k_nope, k_rope, v).reshape(B, S, MLA_HEADS * MLA_V)
    return jnp.concatenate([y_conv, y_att], axis=-1) @ w_out


def nsa_attention(h, w_in, cmp_pe, cmp_w1, cmp_w2, gate_b, w_out):
    B, S, _ = h.shape
    G, R, DK = NSA_KV_HEADS, NSA_GROUP, NSA_DK
    dt = h.dtype
    z = h @ w_in
    qd = NSA_HEADS * DK
    q, kv_c, kv_s, kv_w, g = jnp.split(
        z, [qd, qd + NSA_KV_W, qd + 2 * NSA_KV_W, qd + 3 * NSA_KV_W], axis=-1)
    q = q.reshape(B, S, G, R, DK)
    gates = jax.nn.sigmoid(g + gate_b).reshape(B, S, G, R, 3)
    kv_c = kv_c.reshape(B, S, 2, G, DK)
    kv_s = kv_s.reshape(B, S, 2, G, DK)
    kv_w = kv_w.reshape(B, S, 2, G, DK)

    nc = (S - CMP_BLOCK) // CMP_STRIDE + 1
    cidx = jnp.arange(nc)[:, None] * CMP_STRIDE + jnp.arange(CMP_BLOCK)[None, :]

    def compress(t, pe, w1, w2):
        blk = t[:, cidx] + pe[:, None, :]
        blk = jnp.moveaxis(blk, 3, 2).reshape(B, nc, G, CMP_BLOCK * DK)
        return jax.nn.silu(blk @ w1) @ w2

    k_cmp = compress(kv_c[:, :, 0], cmp_pe[0], cmp_w1[0], cmp_w2[0])
    v_cmp = compress(kv_c[:, :, 1], cmp_pe[1], cmp_w1[1], cmp_w2[1])
    cmp_end = jnp.arange(nc) * CMP_STRIDE + CMP_BLOCK - 1

    ns = S // SLC_BLOCK
    n_sel = min(N_SEL, ns)
    cs = jnp.arange(nc)[:, None] * CMP_STRIDE
    ss = jnp.arange(ns)[None, :] * SLC_BLOCK
    overlap = jnp.clip(jnp.minimum(cs + CMP_BLOCK, ss + SLC_BLOCK) - jnp.maximum(cs, ss), 0, None)
    agg = overlap.astype(jnp.float32) / CMP_BLOCK
    blk_idx = jnp.arange(ns)
    k_slc = kv_s[:, :, 0].reshape(B, ns, SLC_BLOCK, G, DK).transpose(0, 3, 1, 2, 4)
    v_slc = kv_s[:, :, 1].reshape(B, ns, SLC_BLOCK, G, DK).transpose(0, 3, 1, 2, 4)
    bi = jnp.arange(B)[:, None, None, None]
    gi = jnp.arange(G)[None, :, None, None]

    pad = ((0, 0), (WINDOW, 0), (0, 0), (0, 0))
    k_win = jnp.pad(kv_w[:, :, 0], pad)
    v_win = jnp.pad(kv_w[:, :, 1], pad)
    scale = DK ** -0.5
    nqb = S // Q_BLOCK

    def block(args):
        i, qb, gb = args
        t = i * Q_BLOCK + jnp.arange(Q_BLOCK)
        s_c = jnp.einsum('bqgrd,bngd->bgrqn', qb, k_cmp) * scale
        p_c = masked_softmax(s_c, cmp_end[None, :] <= t[:, None])
        o_c = jnp.einsum('bgrqn,bngd->bqgrd', p_c.astype(dt), v_cmp)
        imp = jnp.einsum('bgrqn,nj->bgqj', p_c, agg)
        cur = t // SLC_BLOCK
        forced = ((blk_idx[None, :] == 0) | (blk_idx[None, :] == cur[:, None])
                  | (blk_idx[None, :] == cur[:, None] - 1))
        causal_blk = blk_idx[None, :] * SLC_BLOCK <= t[:, None]
        imp = jnp.where(forced, FORCE_SCORE, jnp.where(causal_blk, imp, NEG_INF))
        _, sel = lax.top_k(imp, n_sel)
        ks = k_slc[bi, gi, sel].reshape(B, G, Q_BLOCK, n_sel * SLC_BLOCK, DK)
        vs = v_slc[bi, gi, sel].reshape(B, G, Q_BLOCK, n_sel * SLC_BLOCK, DK)
        kpos_s = (sel[..., None] * SLC_BLOCK + jnp.arange(SLC_BLOCK)).reshape(B, G, Q_BLOCK, -1)
        s_s = jnp.einsum('bqgrd,bgqkd->bgrqk', qb, ks) * scale
        p_s = masked_softmax(s_s, kpos_s[:, :, None] <= t[:, None])
        o_s = jnp.einsum('bgrqk,bgqkd->bqgrd', p_s.astype(dt), vs)
        kw = lax.dynamic_slice_in_dim(k_win, i * Q_BLOCK, WINDOW + Q_BLOCK, axis=1)
        vw = lax.dynamic_slice_in_dim(v_win, i * Q_BLOCK, WINDOW + Q_BLOCK, axis=1)
        kpos_w = i * Q_BLOCK - WINDOW + jnp.arange(WINDOW + Q_BLOCK)
        mask_w = ((kpos_w[None, :] <= t[:, None]) & (kpos_w[None, :] > t[:, None] - WINDOW)
                  & (kpos_w[None, :] >= 0))
        s_w = jnp.einsum('bqgrd,bkgd->bgrqk', qb, kw) * scale
        p_w = masked_softmax(s_w, mask_w)
        o_w = jnp.einsum('bgrqk,bkgd->bqgrd', p_w.astype(dt), vw)
        return gb[..., 0:1] * o_c + gb[..., 1:2] * o_s + gb[..., 2:3] * o_w

    o = lax.map(block, (jnp.arange(nqb), to_blocks(q, nqb), to_blocks(gates, nqb)))
    return from_blocks(o).reshape(B, S, NSA_HEADS * DK) @ w_out


def setup_inputs(seed: int = 0) -> dict:
    key = jax.random.key(seed)
    ks = jax.random.split(key, 24)
    nrm = lambda k, shape, s: jax.random.normal(k, shape, jnp.float32) * s
    D = D_MODEL
    return {
        "x": nrm(ks[0], (BATCH, SEQ, D), 1.0),
        "c": nrm(ks[1], (BATCH, D), 1.0),
        "positions": (jnp.arange(SEQ, dtype=jnp.int32)[None, :]
                      + jax.random.randint(ks[2], (BATCH, 1), 0, 1024, dtype=jnp.int32)),
        "ada_w": nrm(ks[3], (DEPTH, D, N_ADA * D), 0.5 * D ** -0.5),
        "ada_b": nrm(ks[4], (DEPTH, N_ADA * D), 0.02),
        "norm_g": 1.0 + nrm(ks[5], (DEPTH, 3, D), 0.02),
        "final_g": 1.0 + nrm(ks[6], (D,), 0.02),
        "ff_w13": nrm(ks[7], (DEPTH, 2, D, 2 * D_FF), D ** -0.5),
        "ff_w2": nrm(ks[8], (DEPTH, 2, D_FF, D), D_FF ** -0.5),
        "hy_w_in": nrm(ks[9], (N_EVEN, D, HY_IN), D ** -0.5),
        "hy_conv_w": nrm(ks[10], (N_EVEN, CONV_TAPS, CONV_WIDTH), CONV_TAPS ** -0.5),
        "hy_q_norm": 1.0 + nrm(ks[11], (N_EVEN, Q_LORA), 0.02),
        "hy_kv_norm": 1.0 + nrm(ks[12], (N_EVEN, KV_LORA), 0.02),
        "hy_w_uq": nrm(ks[13], (N_EVEN, Q_LORA, MLA_HEADS * (MLA_NOPE + MLA_ROPE)), Q_LORA ** -0.5),
        "hy_w_ukv": nrm(ks[14], (N_EVEN, KV_LORA, MLA_HEADS * (MLA_NOPE + MLA_V)), KV_LORA ** -0.5),
        "hy_w_out": nrm(ks[15], (N_EVEN, HY_MIX, D), HY_MIX ** -0.5),
        "nsa_w_in": nrm(ks[16], (N_ODD, D, NSA_IN), D ** -0.5),
        "nsa_cmp_pe": nrm(ks[17], (N_ODD, 2, CMP_BLOCK, NSA_DK), 0.1),
        "nsa_cmp_w1": nrm(ks[18], (N_ODD, 2, CMP_BLOCK * NSA_DK, CMP_HID), (CMP_BLOCK * NSA_DK) ** -0.5),
        "nsa_cmp_w2": nrm(ks[19], (N_ODD, 2, CMP_HID, NSA_DK), CMP_HID ** -0.5),
        "nsa_gate_b": nrm(ks[20], (N_ODD, 3 * NSA_HEADS), 0.1),
        "nsa_w_out": nrm(ks[21], (N_ODD, NSA_HEADS * NSA_DK, D), (NSA_HEADS * NSA_DK) ** -0.5),
    }


def reference(x, c, positions, ada_w, ada_b, norm_g, final_g, ff_w13, ff_w2,
              hy_w_in, hy_conv_w, hy_q_norm, hy_kv_norm, hy_w_uq, hy_w_ukv, hy_w_out,
              nsa_w_in, nsa_cmp_pe, nsa_cmp_w1, nsa_cmp_w2, nsa_gate_b, nsa_w_out):
    cos, sin = rope_tables(positions)
    c_act = jax.nn.silu(c)
    for l in range(DEPTH):
        mod = c_act @ ada_w[l] + ada_b[l]
        sh1, sc1, g1, sh2, sc2, g2, sh3, sc3, g3 = jnp.split(mod, N_ADA, axis=-1)
        h = modulate(x, norm_g[l, 0], sh1, sc1)
        x = x + 0.5 * g1[:, None] * swiglu(h, ff_w13[l, 0], ff_w2[l, 0])
        h = modulate(x, norm_g[l, 1], sh2, sc2)
        m = l // 2
        if l % 2 == 0:
            y = hybrid_conv_mla(h, cos, sin, hy_w_in[m], hy_conv_w[m], hy_q_norm[m],
                                hy_kv_norm[m], hy_w_uq[m], hy_w_ukv[m], hy_w_out[m])
        else:
            y = nsa_attention(h, nsa_w_in[m], nsa_cmp_pe[m], nsa_cmp_w1[m], nsa_cmp_w2[m],
                              nsa_gate_b[m], nsa_w_out[m])
        x = x + g2[:, None] * y
        h = modulate(x, norm_g[l, 2], sh3, sc3)
        x = x + 0.5 * g3[:, None] * swiglu(h, ff_w13[l, 1], ff_w2[l, 1])
    return rms_norm(x, final_g)
```

```python
import numpy as np
import ml_dtypes
import concourse.bass as bass
import concourse.mybir as mybir
from concourse.bass_utils import run_bass_kernel_spmd

F32 = mybir.dt.float32
BF16 = mybir.dt.bfloat16
I32 = mybir.dt.int32
AF = mybir.ActivationFunctionType
ALU = mybir.AluOpType

D = 1024
S = 2048
DFF = 2816
NFC = 22
DEPTH = 4
EPS = 1e-6
NEG = -30000.0
PI = float(np.pi)

import os
NSA_BR = tuple(int(c) for c in os.environ.get("NSA_BR", "012"))
COMPUTE = ("pe", "dve", "act", "pool")
QUEUES = ("sp",)


class Res:
    __slots__ = ("name", "w", "r", "sem", "cnt")

    def __init__(self, name):
        self.name = name
        self.w = None
        self.r = {}
        self.sem = None
        self.cnt = 0


class Prog:
    def __init__(self, nc):
        self.nc = nc
        self.streams = {e: [] for e in ("pe", "dve", "act", "pool", "sp")}
        self.count = {e: 0 for e in COMPUTE}
        self.seen = {e: {} for e in self.streams}
        self.esem = {}
        self.ctx = []
        self.nsem = 0

    def enter(self, cm):
        v = cm.__enter__()
        self.ctx.append(cm)
        return v

    def sb(self, name, shape, dt):
        return self.enter(self.nc.sbuf_tensor("sb_" + name, list(shape), dt))

    def ps(self, name, shape, dt=F32):
        return self.enter(self.nc.psum_tensor(name, list(shape), dt))

    def new_sem(self, name):
        self.nsem += 1
        return self.enter(self.nc.semaphore(f"{name}_{self.nsem}"))

    def setup(self):
        for e in COMPUTE:
            self.esem[e] = self.new_sem("es_" + e)

    def _need(self, eng, tok, waits):
        if tok is None:
            return
        if tok[0] == "dma":
            _, res, cnt = tok
            key = ("dma", id(res))
            if self.seen[eng].get(key, 0) >= cnt:
                return
            self.seen[eng][key] = cnt
            waits[key] = (res.sem, 16 * cnt)
        else:
            f, idx = tok
            if f == eng and f == "pe":
                return
            if self.seen[eng].get(f, 0) >= idx:
                return
            self.seen[eng][f] = idx
            waits[f] = (self.esem[f], idx)

    def _deps(self, eng, reads, writes, skip_res=None):
        waits = {}
        for r in reads:
            self._need(eng, r.w, waits)
        for w in writes:
            t = w.w
            if t is not None and not (t[0] == eng) and not (t[0] == "dma" and t[1] is skip_res):
                self._need(eng, t, waits)
            for tok in w.r.values():
                if tok[0] == eng:
                    continue
                self._need(eng, tok, waits)
        return list(waits.values())

    def op(self, eng, fn, reads=(), writes=()):
        waits = self._deps(eng, reads, writes)
        self.count[eng] += 1
        idx = self.count[eng]
        sem = self.esem[eng]

        def emit(h, fn=fn, waits=waits, sem=sem):
            for (s, v) in waits:
                h.wait_ge(s, v)
            fn(h).then_inc(sem, 1)
        self.streams[eng].append(emit)
        tok = (eng, idx)
        for r in reads:
            r.r[eng] = tok
        for w in writes:
            w.w = tok
            w.r = {}
        return tok

    def dma(self, q, fn, reads=(), writes=(), track=None):
        tr = track if track is not None else (writes[0] if writes else reads[0])
        waits = self._deps(q, reads, writes, skip_res=tr)
        if tr.sem is None:
            tr.sem = self.new_sem("ds_" + tr.name)
        tr.cnt += 1
        tok = ("dma", tr, tr.cnt)
        sem = tr.sem

        def emit(h, fn=fn, waits=waits, sem=sem):
            for (s, v) in waits:
                h.wait_ge(s, v)
            fn(h).then_inc(sem, 16)
        self.streams[q].append(emit)
        for r in reads:
            r.r[("dma", id(tr))] = tok
        for w in writes:
            w.w = tok
            w.r = {}
        return tok

    def barrier(self):
        for e in self.streams:
            wl = []
            for f in COMPUTE:
                c = self.count[f]
                if c > 0 and self.seen[e].get(f, 0) < c:
                    self.seen[e][f] = c
                    wl.append((self.esem[f], c))
            if wl:
                def emit(h, wl=wl):
                    for (s, v) in wl:
                        h.wait_ge(s, v)
                self.streams[e].append(emit)

    def wait_tokens(self, q, toks):
        waits = {}
        for t in toks:
            self._need(q, t, waits)
        wl = list(waits.values())

        def emit(h, wl=wl):
            for (s, v) in wl:
                h.wait_ge(s, v)
        self.streams[q].append(emit)

    def run_block(self):
        nc = self.nc
        with nc.Block() as block:
            @block.tensor
            def _(h):
                for f in self.streams["pe"]:
                    f(h)

            @block.vector
            def _(h):
                for f in self.streams["dve"]:
                    f(h)

            @block.scalar
            def _(h):
                for f in self.streams["act"]:
                    f(h)

            @block.gpsimd
            def _(h):
                for f in self.streams["pool"]:
                    f(h)

            @block.sync
            def _(h):
                for f in self.streams["sp"]:
                    f(h)

    def close(self):
        for cm in reversed(self.ctx):
            cm.__exit__(None, None, None)
        self.ctx = []


class Rot:
    def __init__(self, items):
        self.items = items
        self.i = 0

    def next(self):
        it = self.items[self.i % len(self.items)]
        self.i += 1
        return it


def _c(a):
    return np.ascontiguousarray(a)


def _pk(w):
    k = w.shape[0] // 128
    return _c(w.reshape(k, 128, w.shape[1]).transpose(1, 0, 2))


def host_consts():
    bf = ml_dtypes.bfloat16
    cst = {}
    cst["ident"] = np.eye(128, dtype=np.float32).astype(bf)
    cst["ones"] = np.ones((128, 128), np.float32).astype(bf)
    p = np.arange(128)[:, None]
    a = np.arange(512)[None, :]
    mt = np.zeros((128, 8, 512), np.float32)
    for m in range(4):
        mt[:, m, :] = np.where(128 * m + p > a, 0.0, NEG)
        mt[:, 4 + m, :] = np.where(128 * m + p <= a, 0.0, NEG)
    cst["masks"] = mt.astype(bf)
    n = np.arange(128)[:, None]
    t = np.arange(S)[None, :]
    cst["cmask"] = np.where((16 * n + 31 <= t) & (n < 127), 0.0, NEG).astype(np.float32).astype(bf)
    cs = np.arange(127)[:, None] * 16
    ss = np.arange(32)[None, :] * 64
    ov = np.clip(np.minimum(cs + 32, ss + 64) - np.maximum(cs, ss), 0, None).astype(np.float32) / 32.0
    agg = np.zeros((128, 33), np.float32)
    agg[:127, :32] = ov
    agg[:127, 32] = 1.0
    cst["agg"] = agg.astype(bf)
    tt = np.arange(S)[:, None]
    j = np.arange(32)[None, :]
    cur = tt // 64
    forced = (j == 0) | (j == cur) | (j == cur - 1)
    causal = (j * 64 <= tt)
    A = ((~forced) & causal).astype(np.float32)
    B = np.where(forced, 1e4, np.where(causal, 0.0, -1e30)).astype(np.float32)
    cst["selA"] = _c(A.reshape(16, 128, 32).transpose(1, 0, 2))
    cst["selB"] = _c(B.reshape(16, 128, 32).transpose(1, 0, 2))
    g = np.zeros((48, 48, 64), np.float32)
    for r in range(48):
        g[r, r, :] = 1.0
    cst["gsel"] = g.astype(bf)
    b2 = np.zeros((32, 16, 128), np.float32)
    for kt in range(16):
        b2[2 * kt, kt, :64] = 1.0
        b2[2 * kt + 1, kt, 64:] = 1.0
    cst["bsel"] = b2.astype(bf)
    oh = np.zeros((32, S), np.float32)
    oh[np.arange(S) // 64, np.arange(S)] = 1.0
    cst["onehot"] = oh.astype(bf)
    half = 16
    inv = (10000.0 ** (-np.arange(half, dtype=np.float32) / half)).astype(np.float32)
    rc = np.zeros((128, 4), np.float32)
    rc[64:80, 0] = inv
    rc[80:96, 0] = inv
    rc[64:80, 1] = -1.0
    rc[80:96, 1] = 1.0
    rc[:, 2] = -PI * rc[:, 1]
    rc[:, 3] = -PI
    cst["ropec"] = rc
    fc = np.zeros((128, 8), np.float32)
    fc[:, 4] = 1e-18
    fc[:, 0] = D * EPS
    fc[:, 1] = 256 * EPS
    fc[:, 2] = 128 * EPS
    fc[:, 3] = 1.0
    cst["fconst"] = fc
    return cst


def host_prep(inp):
    f = np.float32
    sh = {}
    aw = np.asarray(inp["ada_w"], f)
    NLh = aw.shape[0]
    sh["ada_w"] = _c(aw.reshape(NLh, 8, 128, 9, 1024).transpose(0, 3, 2, 1, 4))
    sh["ada_b"] = _c(np.asarray(inp["ada_b"], f).reshape(NLh, 72, 128).transpose(2, 0, 1))
    sh["norm_g"] = _c(np.asarray(inp["norm_g"], f).reshape(NLh, 3, 8, 128).transpose(3, 0, 1, 2))
    sh["final_g"] = _c(np.asarray(inp["final_g"], f).reshape(8, 128).T)
    w13 = np.asarray(inp["ff_w13"], f)
    wa = w13[..., :DFF].reshape(NLh, 2, 8, 128, NFC, 128)
    wb = w13[..., DFF:].reshape(NLh, 2, 8, 128, NFC, 128)
    w13r = np.stack([wa, wb], axis=5)
    sh["ff_w13"] = _c(w13r.transpose(0, 1, 4, 3, 2, 5, 6)).reshape(NLh, 2, NFC, 128, 8, 256)
    sh["ff_w2"] = _c(np.asarray(inp["ff_w2"], f).reshape(NLh, 2, NFC, 128, D))
    wi = np.asarray(inp["hy_w_in"], f)
    NE = wi.shape[0]
    conv = []
    for j in range(4):
        cols = np.concatenate([np.arange(j * 128, (j + 1) * 128) + o for o in (0, 512, 1024)])
        conv.append(np.stack([_pk(wi[m][:, cols]) for m in range(NE)]))
    sh["hy_conv_in"] = _c(np.stack(conv, axis=1))
    base = 1536
    cq = np.arange(base, base + 256)
    ckv = np.arange(base + 256, base + 384)
    kr = np.arange(base + 384, base + 416)
    krA = np.concatenate([cq[:64], kr])
    krB = np.concatenate([cq[:64], kr[16:], kr[:16]])
    lat = np.concatenate([cq, ckv, krA, krB])
    sh["hy_lat_in"] = _c(np.stack([_pk(wi[m][:, lat]) for m in range(NE)]))
    sh["hy_conv_w"] = _c(np.asarray(inp["hy_conv_w"], f).reshape(NE, 3, 4, 128).transpose(3, 0, 2, 1))
    sh["hy_q_norm"] = _c(np.asarray(inp["hy_q_norm"], f).reshape(NE, 2, 128).transpose(2, 0, 1))
    sh["hy_kv_norm"] = _c(np.asarray(inp["hy_kv_norm"], f).T)
    uq = np.asarray(inp["hy_w_uq"], f).reshape(NE, 256, 8, 96)
    uqB = np.concatenate([uq[..., :64], uq[..., 80:96], uq[..., 64:80]], axis=-1)
    uq2 = np.stack([uq, uqB], axis=3)
    sh["hy_w_uq"] = _c(uq2.reshape(NE, 2, 128, 8, 2, 96).transpose(0, 2, 1, 3, 4, 5))
    sh["hy_w_ukv"] = _c(np.asarray(inp["hy_w_ukv"], f))
    sh["hy_w_out"] = _c(np.stack([_pk(np.asarray(inp["hy_w_out"], f)[m]) for m in range(NE)]))
    ni = np.asarray(inp["nsa_w_in"], f)
    NO = ni.shape[0]
    sh["nsa_q_in"] = _c(np.stack([_pk(ni[m][:, :1024]) for m in range(NO)]))
    o = 1024
    kc = np.arange(o, o + 128)
    vc = np.arange(o + 128, o + 256)
    ks = [np.arange(o + 256 + g * 64, o + 256 + (g + 1) * 64) for g in range(2)]
    vs = np.arange(o + 256 + 128, o + 512)
    kw = [np.arange(o + 512 + g * 64, o + 512 + (g + 1) * 64) for g in range(2)]
    vw = np.arange(o + 512 + 128, o + 768)
    gc = np.arange(o + 768, o + 816)
    st = np.concatenate([kc, vc, ks[0], ks[1], kw[0], kw[1], gc])
    sh["nsa_st_in"] = _c(np.stack([_pk(ni[m][:, st]) for m in range(NO)]))
    mv = np.concatenate([vs, vw])
    sh["nsa_v_in"] = _c(np.stack([_pk(ni[m][:, mv]) for m in range(NO)]))
    w1 = np.asarray(inp["nsa_cmp_w1"], f).reshape(NO, 2, 32, 64, 128)
    w1 = w1.transpose(0, 1, 3, 2, 4)
    sh["nsa_w1"] = _c(np.concatenate([w1, w1], axis=2).transpose(0, 2, 1, 3, 4))
    w2 = np.asarray(inp["nsa_cmp_w2"], f)
    sh["nsa_w2"] = _c(np.concatenate([w2[:, 0], w2[:, 0], w2[:, 1]], axis=-1))
    pe = np.asarray(inp["nsa_cmp_pe"], f)
    pet = pe.transpose(0, 1, 3, 2)
    sh["nsa_pe"] = _c(np.concatenate([pet, pet], axis=2).transpose(0, 2, 1, 3))
    sh["nsa_gate_b"] = _c(np.asarray(inp["nsa_gate_b"], f)[:, :, None])
    sh["nsa_w_out"] = _c(np.stack([_pk(np.asarray(inp["nsa_w_out"], f)[m]) for m in range(NO)]))
    sh.update(host_consts())
    return sh


def host_core(inp, b):
    pc = {}
    x = np.asarray(inp["x"], np.float32)[b]
    pc["xT"] = _c(x.T.reshape(8, 128, S).transpose(1, 0, 2))
    pc["cT"] = _c(np.asarray(inp["c"], np.float32)[b].reshape(8, 128).T)
    pos = np.asarray(inp["positions"], np.int32)[b]
    pc["pos"] = _c(np.broadcast_to(pos[None, :], (32, S)))
    return pc


def build_program(shapes, lmap=((0, 0, 0), (1, 1, 0), (2, 0, 1), (3, 1, 1)), do_mixer=True, do_ffn=True,
                  do_final=True, x_in_name="xT"):
    layers = [t[0] for t in lmap]
    nc = bass.Bass("TRN2", target_bir_lowering=False)
    dr = {}
    for name, (shp, dt) in shapes.items():
        dr[name] = nc.dram_tensor(name, list(shp), dt, kind="ExternalInput").ap()
    outT = nc.dram_tensor("outT", [128, 8, S], F32, kind="ExternalOutput").ap()
    DBG = {}
    if os.environ.get("NSA_DBG"):
        DBG["kTc"] = nc.dram_tensor("dbg_kTc", [128, 128], BF16, kind="ExternalOutput").ap()
        DBG["Vc"] = nc.dram_tensor("dbg_Vc", [2, 128, 128], BF16, kind="ExternalOutput").ap()
        DBG["negT"] = nc.dram_tensor("dbg_negT", [2, 32, 2048], BF16, kind="ExternalOutput").ap()
        DBG["LNG"] = nc.dram_tensor("dbg_LNG", [48, 2048], BF16, kind="ExternalOutput").ap()
        DBG["imp"] = nc.dram_tensor("dbg_imp", [2, 128, 512], F32, kind="ExternalOutput").ap()
        DBG["klo"] = nc.dram_tensor("dbg_klo", [128, 2048], BF16, kind="ExternalOutput").ap()
    Rdbg = Res("dbg")

    P = Prog(nc)
    P.setup()
    op, dma = P.op, P.dma

    X = P.sb("X", [128, 8, S], F32)
    RX = [[Res(f"X{k}_{tb}") for tb in range(4)] for k in range(8)]
    HT_flat = P.sb("HT", [128, 8 * S], BF16)
    HT = HT_flat[:].rearrange("p (k t) -> p k t", t=S)
    RH = [Res(f"H{tb}") for tb in range(4)]
    YT = P.sb("YT", [128, 16 * 1024], BF16)
    WA = P.sb("WA", [128, 24 * 1024], BF16)
    ident = P.sb("ident", [128, 128], BF16)
    ones = P.sb("ones", [128, 128], BF16)
    masks = P.sb("masks", [128, 8, 512], BF16)
    NL = max(1, len(layers))
    MODT = P.sb("MODT", [128, NL, 72], F32)
    ADAB = P.sb("ADAB", [128, NL, 72], F32)
    NG = P.sb("NG", [128, NL, 3, 8], F32)
    GM = P.sb("GM", [128, NL, 3, 8], F32)
    GC = P.sb("GC", [128, NL, 3, 8], F32)
    FG = P.sb("FG", [128, 8], F32)
    CT = P.sb("CT", [128, 8], F32)
    FC = P.sb("FC", [128, 8], F32)
    CA = P.sb("CA", [128, 8], BF16)
    RS = P.sb("RS", [128, 2, 512], F32)
    TMP = P.sb("TMP", [128, 2, 512], F32)
    Rconst = Res("const")
    Rmod = Res("mod")

    PSB = [P.ps(f"psb{i}", [128, 512], F32) for i in range(8)]
    RPS = [Res(f"ps{i}") for i in range(8)]

    def arena(t, off_bytes, shape, dt):
        n = int(np.prod(shape[1:]))
        if dt == F32:
            v = t[:, off_bytes // 2: off_bytes // 2 + 2 * n].bitcast(F32)
        else:
            v = t[:, off_bytes // 2: off_bytes // 2 + n]
        if len(shape) == 3:
            v = v.rearrange("p (a b) -> p a b", b=shape[2])
        elif len(shape) == 4:
            v = v.rearrange("p (a b c) -> p a b c", b=shape[2], c=shape[3])
        return v

    for k in range(8):
        dma("sp", lambda h, k=k: h.dma_start(out=X[:, k, :], in_=dr[x_in_name][:, k, :]), writes=RX[k])
    for (dst, nm) in ((ident, "ident"), (ones, "ones"), (masks, "masks"), (ADAB, "ada_b"), (NG, "norm_g"),
                      (FG, "final_g"), (CT, "cT"), (FC, "fconst")):
        dma("sp", lambda h, dst=dst, nm=nm: h.dma_start(out=dst[:], in_=dr[nm]), writes=[Rconst])

    op("act", lambda h: h.activation(out=CA[:], in_=CT[:], func=AF.Silu), reads=[Rconst], writes=[Rmod])

    if len(layers) > 0:
        adaslots = [(arena(WA, i * 16384, [128, 8, 1024], BF16), Res(f"adas{i}")) for i in range(3)]
        rot = Rot(adaslots)
        mps = PSB[7]
        for l in layers:
            for j in range(9):
                sl, rs_ = rot.next()
                for kk in range(0, 8, 4):
                    dma("pool", lambda h, sl=sl, l=l, j=j, kk=kk: h.dma_start(
                        out=sl[:, kk:kk + 4, :], in_=dr["ada_w"][l, j, :, kk:kk + 4, :]), writes=[rs_])
                for cc in range(8):
                    col = l * 72 + j * 8 + cc
                    for k in range(8):
                        op("pe", lambda h, sl=sl, cc=cc, k=k, col=col: h.matmul(
                            mps[:, col:col + 1], lhsT=sl[:, k, cc * 128:(cc + 1) * 128], rhs=CA[:, k:k + 1],
                            start=(k == 0), stop=(k == 7)), reads=[rs_, Rmod], writes=[RPS[7]])
        l0, l1 = min(layers), max(layers) + 1
        op("dve", lambda h: h.tensor_tensor(
            out=MODT[:, l0:l1, :], in0=mps[:, l0 * 72:l1 * 72].rearrange("p (l c) -> p l c", c=72),
            in1=ADAB[:, l0:l1, :], op=ALU.add), reads=[RPS[7], Rconst], writes=[Rmod])
        for l in layers:
            for i in range(3):
                sc = MODT[:, l, (3 * i + 1) * 8:(3 * i + 2) * 8]
                gt = MODT[:, l, (3 * i + 2) * 8:(3 * i + 3) * 8]
                op("dve", lambda h, l=l, i=i, sc=sc: h.scalar_tensor_tensor(
                    out=GM[:, l, i, :], in0=sc, scalar=1.0, in1=NG[:, l, i, :], op0=ALU.add, op1=ALU.mult),
                   reads=[Rmod, Rconst], writes=[Rmod])
                op("dve", lambda h, l=l, i=i: h.tensor_scalar(
                    out=GM[:, l, i, :], in0=GM[:, l, i, :], scalar1=32.0, scalar2=None, op0=ALU.mult),
                   reads=[Rmod], writes=[Rmod])
                op("dve", lambda h, l=l, i=i, gt=gt: h.tensor_scalar(
                    out=GC[:, l, i, :], in0=gt, scalar1=(1.0 if i == 1 else 0.5), scalar2=None, op0=ALU.mult),
                   reads=[Rmod], writes=[Rmod])
        P.barrier()

    rs_rot = Rot([(RS[:, i, :], Res(f"rs{i}")) for i in range(2)])
    tmp_rot = Rot([(TMP[:, i, :], Res(f"tmp{i}")) for i in range(2)])
    SQ = arena(YT, 0, [128, 8, 512], BF16)
    RSQ = Res("sq")

    def norm_stats(tb):
        tsl = slice(tb * 512, (tb + 1) * 512)
        op("act", lambda h: h.activation(out=SQ[:], in_=X[:, :, tsl], func=AF.Square),
           reads=[RX[k][tb] for k in range(8)], writes=[RSQ])
        for k in range(8):
            op("pe", lambda h, k=k: h.matmul(PSB[6][:], lhsT=ones[:], rhs=SQ[:, k, :], start=(k == 0), stop=(k == 7)),
               reads=[RSQ, Rconst], writes=[RPS[6]])
        rs, rrs = rs_rot.next()
        op("act", lambda h: h.activation(out=rs, in_=PSB[6][:], func=AF.Ln, bias=FC[:, 0:1]),
           reads=[RPS[6], Rconst], writes=[rrs])
        op("act", lambda h: h.activation(out=rs, in_=rs, func=AF.Exp, scale=-0.5), reads=[rrs], writes=[rrs])
        return rs, rrs

    def mod_norm(l, i):
        for tb in range(4):
            mod_norm_tb(l, i, tb)

    def mod_norm_tb(l, i, tb):
        if True:
            tsl = slice(tb * 512, (tb + 1) * 512)
            rs, rrs = norm_stats(tb)
            for k in range(8):
                tm, rtm = tmp_rot.next()
                op("dve", lambda h, k=k, tm=tm: h.scalar_tensor_tensor(
                    out=tm, in0=X[:, k, tsl], scalar=GM[:, l, i, k:k + 1], in1=rs, op0=ALU.mult, op1=ALU.mult),
                   reads=[RX[k][tb], rrs, Rmod], writes=[rtm])
                op("act", lambda h, k=k, tm=tm: h.activation(
                    out=HT[:, k, tsl], in_=tm, func=AF.Identity, bias=MODT[:, l, 3 * i * 8 + k:3 * i * 8 + k + 1]),
                   reads=[rtm, Rmod], writes=[RH[tb]])

    GROUPS = [(0, 4), (4, 8), (8, 12), (12, 16), (16, 20), (20, 22)]
    w13s = [arena(WA, s * 24576, [128, 4, 8, 256], BF16) for s in range(2)]
    w2s = [arena(WA, s * 24576 + 16384, [128, 4, 1024], BF16) for s in range(2)]
    Rw = [Res("wslot0"), Res("wslot1")]
    GB = [arena(YT, 8192 + s * 4096, [128, 4, 512], BF16) for s in range(2)]
    RG = [Res("g0"), Res("g1")]
    SA = [arena(YT, 16384 + s * 2048, [128, 512], F32) for s in range(2)]
    RSA = [Res("sa0"), Res("sa1")]
    ffn_state = {"slot": 0}

    def ffn_load(l, i, gi):
        s = ffn_state["slot"] % 2
        ffn_state["slot"] += 1
        f0, f1 = GROUPS[gi]
        n = f1 - f0
        for a in range(0, n, 2):
            dma("pool", lambda h, s=s, a=a: h.dma_start(
                out=w13s[s][:, a:a + 2, :, :], in_=dr["ff_w13"][l, i // 2, f0 + a:f0 + a + 2].rearrange("f p k c -> p f k c")),
                writes=[Rw[s]])
        dma("pool", lambda h, s=s: h.dma_start(
            out=w2s[s][:, 0:n, :], in_=dr["ff_w2"][l, i // 2, f0:f1].rearrange("f p c -> p f c")), writes=[Rw[s]])
        return s

    def ffn(l, i, preloaded):
        units = [(gi, tb) for gi in range(len(GROUPS)) for tb in range(4)]
        slots = dict(preloaded)
        ab_rot = Rot([(0, 1), (2, 3)])
        o_rot = Rot([4, 5])
        pend = None
        ui = 0

        def m2(gi, tb, s, gb):
            n = GROUPS[gi][1] - GROUPS[gi][0]
            tsl = slice(tb * 512, (tb + 1) * 512)
            for dc in range(8):
                ob = o_rot.next()
                for fi in range(n):
                    op("pe", lambda h, fi=fi, dc=dc, ob=ob: h.matmul(
                        PSB[ob][:], lhsT=w2s[s][:, fi, dc * 128:(dc + 1) * 128], rhs=GB[gb][:, fi, :],
                        start=(fi == 0), stop=(fi == n - 1)), reads=[Rw[s], RG[gb]], writes=[RPS[ob]])
                op("dve", lambda h, dc=dc, ob=ob: h.scalar_tensor_tensor(
                    out=X[:, dc, tsl], in0=PSB[ob][:], scalar=GC[:, l, i, dc:dc + 1], in1=X[:, dc, tsl],
                    op0=ALU.mult, op1=ALU.add), reads=[RPS[ob], Rmod, RX[dc][tb]], writes=[RX[dc][tb]])

        def m1(gi, tb, s, gb):
            n = GROUPS[gi][1] - GROUPS[gi][0]
            tsl = slice(tb * 512, (tb + 1) * 512)
            for fi in range(n):
                pa, pb = ab_rot.next()
                for k in range(8):
                    op("pe", lambda h, fi=fi, k=k, pa=pa: h.matmul(
                        PSB[pa][:], lhsT=w13s[s][:, fi, k, 0:128], rhs=HT[:, k, tsl], start=(k == 0), stop=(k == 7)),
                       reads=[Rw[s], RH[tb]], writes=[RPS[pa]])
                for k in range(8):
                    op("pe", lambda h, fi=fi, k=k, pb=pb: h.matmul(
                        PSB[pb][:], lhsT=w13s[s][:, fi, k, 128:256], rhs=HT[:, k, tsl], start=(k == 0), stop=(k == 7)),
                       reads=[Rw[s], RH[tb]], writes=[RPS[pb]])
                sa = fi % 2
                op("act", lambda h, pa=pa, sa=sa: h.activation(out=SA[sa], in_=PSB[pa][:], func=AF.Silu),
                   reads=[RPS[pa]], writes=[RSA[sa]])
                op("dve", lambda h, pb=pb, sa=sa, fi=fi, gb=gb: h.tensor_tensor(
                    out=GB[gb][:, fi, :], in0=SA[sa], in1=PSB[pb][:], op=ALU.mult),
                   reads=[RSA[sa], RPS[pb]], writes=[RG[gb]])

        for (gi, tb) in units:
            s = slots[gi]
            gb = ui % 2
            ui += 1
            m1(gi, tb, s, gb)
            if pend is not None:
                m2(*pend)
            pend = (gi, tb, s, gb)
            if tb == 0:
                if gi + 1 < len(GROUPS) and (gi + 1) not in slots:
                    slots[gi + 1] = ffn_load(l, i, gi + 1)
        m2(*pend)

    def bview(t):
        return t[:].rearrange("p a b -> p (a b)").bitcast(BF16)

    s_rot = Rot([0, 1, 2])
    o_rot = Rot([3, 4])
    o2_rot = Rot([7, 5])

    def attn_block(q_ap, k_fn, v_fn, tiles, scale, ob, rq, rk, rv, pt_rot, extra_reads=()):
        nt = len(tiles)
        assert tiles[0][2] == 0 and tiles[0][3] == 512

        def pv(idx, kt, pt, rpt, c0, c1):
            op("pe", lambda h: h.matmul(PSB[ob][:, c0:c1], lhsT=v_fn(kt), rhs=pt[:, c0:c1], start=(idx == 0),
                                        stop=(idx == nt - 1)), reads=[rv, rpt], writes=[RPS[ob]])

        def one(idx, kt, mask, c0, c1):
            sbk = s_rot.next()
            mm = [(k_fn(kt), q_ap[:, c0:c1])]
            if mask is not None:
                mm.append((ident[:], mask[:, c0:c1]))
            for jj, (lh, rh) in enumerate(mm):
                op("pe", lambda h, lh=lh, rh=rh, jj=jj: h.matmul(PSB[sbk][:, c0:c1], lhsT=lh, rhs=rh, start=(jj == 0),
                                                               stop=(jj == len(mm) - 1)),
                   reads=[rq, rk, Rconst] + list(extra_reads), writes=[RPS[sbk]])
            pt, rpt = pt_rot.next()
            op("act", lambda h: h.activation(out=pt[:, c0:c1], in_=PSB[sbk][:, c0:c1], func=AF.Exp, scale=float(scale)),
               reads=[RPS[sbk]], writes=[rpt])
            return (idx, kt, pt, rpt, c0, c1)

        pend_pv = []
        for idx, (kt, mask, c0, c1) in enumerate(tiles):
            pend_pv.append(one(idx, kt, mask, c0, c1))
            if len(pend_pv) > 2:
                pv(*pend_pv.pop(0))
            if PEND and (idx == 0 or idx == 2 or idx == nt - 1):
                while PEND and (PEND[0][0] == 0 or idx >= 2 or idx == nt - 1):
                    PEND.pop(0)[1]()
                    if idx == 0 and nt > 3:
                        break
        while pend_pv:
            pv(*pend_pv.pop(0))

    PEND = []

    def flush_pending():
        while PEND:
            PEND.pop(0)[1]()

    def causal_tiles(qb):
        t = [(kt, None, 0, 512) for kt in range(4 * qb)]
        t += [(4 * qb + m, masks[:, 4 + m, :], 128 * m, 512) for m in range(4)]
        return t

    def window_tiles(qb):
        t = [(4 * qb + m, masks[:, 4 + m, :], 128 * m, 512) for m in range(4)]
        if qb > 0:
            t += [(4 * qb - 4 + m, masks[:, m, :], 0, 128 * (m + 1)) for m in range(4)]
        return t

    CONVW = P.sb("CONVW", list(shapes["hy_conv_w"][0]), F32)
    QN = P.sb("QN", list(shapes["hy_q_norm"][0]), F32)
    KVN = P.sb("KVN", list(shapes["hy_kv_norm"][0]), F32)
    ROPEC = P.sb("ROPEC", [128, 4], F32)
    for (dst, nm) in ((CONVW, "hy_conv_w"), (QN, "hy_q_norm"), (KVN, "hy_kv_norm"), (ROPEC, "ropec")):
        dma("sp", lambda h, dst=dst, nm=nm: h.dma_start(out=dst[:], in_=dr[nm]), writes=[Rconst])
    op("dve", lambda h: h.tensor_scalar(out=QN[:], in0=QN[:], scalar1=16.0, scalar2=None, op0=ALU.mult),
       reads=[Rconst], writes=[Rconst])
    op("dve", lambda h: h.tensor_scalar(out=KVN[:], in0=KVN[:], scalar1=float(np.sqrt(128.0)), scalar2=None,
                                        op0=ALU.mult), reads=[Rconst], writes=[Rconst])

    def outproj_chunk(l, wout, rwout, kc, ysrc, rysrc):
        def one(tb, dc):
            tsl = slice(tb * 512, (tb + 1) * 512)
            ob2 = o2_rot.next()
            op("pe", lambda h: h.matmul(PSB[ob2][:], lhsT=wout[:, kc, dc * 128:(dc + 1) * 128], rhs=ysrc[:, tsl],
                                        start=True, stop=True), reads=[rwout, rysrc], writes=[RPS[ob2]])
            op("dve", lambda h: h.scalar_tensor_tensor(
                out=X[:, dc, tsl], in0=PSB[ob2][:], scalar=GC[:, l, 1, dc:dc + 1], in1=X[:, dc, tsl],
                op0=ALU.mult, op1=ALU.add), reads=[RPS[ob2], Rmod, RX[dc][tb]], writes=[RX[dc][tb]])
        for tb in range(4):
            for dc in range(8):
                one(tb, dc)

    def mla(l, mi):
        wout = arena(WA, 0, [128, 8, 1024], BF16)
        latw = arena(WA, 16384, [128, 8, 576], BF16)
        convw = arena(WA, 25600, [128, 8, 384], BF16)
        wuq = arena(WA, 31744, [128, 2, 1536], BF16)
        wukv = arena(WA, 37888, [128, 1024], BF16)
        acc = arena(WA, 39936, [128, 512], F32)
        usb = arena(WA, 41984, [128, 512], F32)
        usb_b = arena(WA, 41984, [128, 2, 512], BF16)
        yc = arena(WA, 44032, [128, 2048], BF16)
        T1 = arena(WA, 44032, [128, 512], F32)
        T2 = arena(WA, 46080, [128, 512], F32)
        Vbuf = arena(YT, 0, [128, 2052], F32)
        posi = Vbuf[:, 0:2048].bitcast(I32)
        cqn = arena(YT, 8208, [128, 2, 2048], BF16)
        ckvn = arena(YT, 16400, [128, 2048], BF16)
        KR = arena(YT, 20496, [128, 2048], BF16)
        CC = bview(TMP)
        SS = bview(RS)
        Rwout, Rlat, Rconvw, Rwuq, Rwukv = [Res(n) for n in ("wout", "latw", "convw", "wuq", "wukv")]
        Racc, Rusb, Ryc, RT1, RT2, RV, Rcqn, Rckvn, RKR, Rtab, Rpos = [Res(n) for n in (
            "acc", "usb", "yc", "T1", "T2", "Vbuf", "cqn", "ckvn", "KR", "tab", "pos")]
        for kk in range(0, 8, 2):
            dma("pool", lambda h, kk=kk: h.dma_start(out=latw[:, kk:kk + 2, :], in_=dr["hy_lat_in"][mi, :, kk:kk + 2, :]),
                writes=[Rlat])
        dma("pool", lambda h: h.dma_start(out=wuq[:], in_=dr["hy_w_uq"][mi].rearrange("p c h a d -> p c (h a d)")),
            writes=[Rwuq])
        dma("pool", lambda h: h.dma_start(out=wukv[:], in_=dr["hy_w_ukv"][mi]), writes=[Rwukv])
        for kk in range(0, 8, 2):
            dma("pool", lambda h, kk=kk: h.dma_start(out=wout[:, kk:kk + 2, :], in_=dr["hy_w_out"][mi, :, kk:kk + 2, :]),
                writes=[Rwout])
        dma("sp", lambda h: h.dma_start(out=posi[64:96, :], in_=dr["pos"]), writes=[Rpos])
        op("dve", lambda h: h.memset(CC[0:64, :], 1.0), writes=[Rtab])
        op("dve", lambda h: h.memset(SS[0:64, :], 0.0), writes=[Rtab])

        def rope_cb(cb):
            csl = slice(cb * 512, (cb + 1) * 512)
            r = slice(64, 96)
            op("dve", lambda h: h.tensor_copy(out=acc[r, :], in_=posi[r, csl]), reads=[Rpos], writes=[Racc])
            op("dve", lambda h: h.tensor_scalar(out=acc[r, :], in0=acc[r, :], scalar1=ROPEC[r, 0:1], scalar2=None,
                                                op0=ALU.mult), reads=[Racc, Rconst], writes=[Racc])
            T1i = T1.bitcast(I32)
            for (shift, dst, sc_ap) in ((0.0, SS, ROPEC[r, 1:2]), (0.5 * PI, CC, None)):
                op("dve", lambda h, shift=shift: h.tensor_scalar(out=usb[r, :], in0=acc[r, :], scalar1=float(shift),
                                                                 scalar2=None, op0=ALU.add), reads=[Racc, Rtab], writes=[Rusb])
                op("dve", lambda h: h.tensor_scalar(out=T2[r, :], in0=usb[r, :], scalar1=float(1.0 / (2 * PI)), scalar2=None,
                                                    op0=ALU.mult), reads=[Rusb], writes=[RT2])
                op("dve", lambda h: h.tensor_copy(out=T1i[r, :], in_=T2[r, :]), reads=[RT2], writes=[RT1])
                op("dve", lambda h: h.tensor_copy(out=T2[r, :], in_=T1i[r, :]), reads=[RT1], writes=[RT2])
                op("dve", lambda h: h.scalar_tensor_tensor(out=usb[r, :], in0=T2[r, :], scalar=float(-2 * PI), in1=usb[r, :],
                                                           op0=ALU.mult, op1=ALU.add), reads=[RT2, Rusb], writes=[Rusb])
                op("dve", lambda h: h.tensor_scalar(out=usb[r, :], in0=usb[r, :], scalar1=-3.141592, scalar2=None, op0=ALU.max),
                   reads=[Rusb], writes=[Rusb])
                op("dve", lambda h: h.tensor_scalar(out=usb[r, :], in0=usb[r, :], scalar1=3.141592, scalar2=None, op0=ALU.min),
                   reads=[Rusb], writes=[Rusb])
                if sc_ap is not None:
                    op("act", lambda h, dst=dst, sc_ap=sc_ap: h.activation(out=dst[r, csl], in_=usb[r, :], func=AF.Sin,
                                                                          scale=sc_ap), reads=[Rusb, Rconst], writes=[Rtab])
                else:
                    op("act", lambda h, dst=dst: h.activation(out=dst[r, csl], in_=usb[r, :], func=AF.Sin),
                       reads=[Rusb], writes=[Rtab])
        for cb in range(4):
            rope_cb(cb)

        def stage_b(tb):
            tsl = slice(tb * 512, (tb + 1) * 512)
            for c, (c0, c1, bank, mrows) in enumerate(((0, 128, 0, 128), (128, 256, 1, 128), (256, 384, 2, 128),
                                                      (384, 480, 3, 96), (480, 576, 4, 96))):
                for k in range(8):
                    op("pe", lambda h, k=k, c0=c0, c1=c1, bank=bank, mrows=mrows: h.matmul(
                        PSB[bank][0:mrows, :], lhsT=latw[:, k, c0:c1], rhs=HT[:, k, tsl], start=(k == 0), stop=(k == 7)),
                       reads=[Rlat, RH[tb]], writes=[RPS[bank]])
            for c in range(2):
                op("act", lambda h, c=c: h.activation(out=usb_b[:, c, :], in_=PSB[c][:], func=AF.Square),
                   reads=[RPS[c]], writes=[Rusb])
            for c in range(2):
                op("pe", lambda h, c=c: h.matmul(PSB[5][:], lhsT=ones[:], rhs=usb_b[:, c, :], start=(c == 0), stop=(c == 1)),
                   reads=[Rusb, Rconst], writes=[RPS[5]])
            op("act", lambda h: h.activation(out=acc[:], in_=PSB[5][:], func=AF.Ln, bias=FC[:, 1:2]),
               reads=[RPS[5], Rconst], writes=[Racc])
            op("act", lambda h: h.activation(out=acc[:], in_=acc[:], func=AF.Exp, scale=-0.5), reads=[Racc], writes=[Racc])
            for c in range(2):
                op("dve", lambda h, c=c: h.scalar_tensor_tensor(
                    out=cqn[:, c, tsl], in0=PSB[c][:], scalar=QN[:, mi, c:c + 1], in1=acc[:], op0=ALU.mult, op1=ALU.mult),
                   reads=[RPS[c], Racc, Rconst], writes=[Rcqn])
            op("act", lambda h: h.activation(out=usb_b[:, 0, :], in_=PSB[2][:], func=AF.Square),
               reads=[RPS[2]], writes=[Rusb])
            op("pe", lambda h: h.matmul(PSB[5][:], lhsT=ones[:], rhs=usb_b[:, 0, :], start=True, stop=True),
               reads=[Rusb, Rconst], writes=[RPS[5]])
            op("act", lambda h: h.activation(out=acc[:], in_=PSB[5][:], func=AF.Ln, bias=FC[:, 2:3]),
               reads=[RPS[5], Rconst], writes=[Racc])
            op("act", lambda h: h.activation(out=acc[:], in_=acc[:], func=AF.Exp, scale=-0.5), reads=[Racc], writes=[Racc])
            op("dve", lambda h: h.scalar_tensor_tensor(
                out=ckvn[:, tsl], in0=PSB[2][:], scalar=KVN[:, mi:mi + 1], in1=acc[:], op0=ALU.mult, op1=ALU.mult),
               reads=[RPS[2], Racc, Rconst], writes=[Rckvn])
            r = slice(64, 96)
            op("dve", lambda h: h.tensor_tensor(out=T1[r, :], in0=PSB[3][r, :], in1=CC[r, tsl], op=ALU.mult),
               reads=[RPS[3], Rtab], writes=[RT1])
            op("dve", lambda h: h.tensor_tensor(out=T2[r, :], in0=PSB[4][r, :], in1=SS[r, tsl], op=ALU.mult),
               reads=[RPS[4], Rtab], writes=[RT2])
            op("dve", lambda h: h.tensor_tensor(out=KR[r, tsl], in0=T1[r, :], in1=T2[r, :], op=ALU.add),
               reads=[RT1, RT2], writes=[RKR])
        for tb in range(4):
            stage_b(tb)

        op("dve", lambda h: h.memset(Vbuf[:, 0:2], 0.0), reads=[Rpos], writes=[RV])

        def stage_a(j, tb):
            tsl = slice(tb * 512, (tb + 1) * 512)
            for c in range(3):
                for k in range(8):
                    op("pe", lambda h, k=k, c=c: h.matmul(
                        PSB[c][:], lhsT=convw[:, k, c * 128:(c + 1) * 128], rhs=HT[:, k, tsl], start=(k == 0), stop=(k == 7)),
                       reads=[Rconvw, RH[tb]], writes=[RPS[c]])
            op("act", lambda h: h.activation(out=usb[:], in_=PSB[0][:], func=AF.Copy), reads=[RPS[0]], writes=[Rusb])
            op("dve", lambda h: h.tensor_tensor(out=Vbuf[:, 2 + tb * 512:2 + (tb + 1) * 512], in0=PSB[1][:], in1=usb[:],
                                                op=ALU.mult), reads=[RPS[1], Rusb], writes=[RV])
            b0 = tb * 512
            op("dve", lambda h: h.tensor_scalar(out=acc[:], in0=Vbuf[:, b0:b0 + 512], scalar1=CONVW[:, mi, j, 0:1],
                                                scalar2=None, op0=ALU.mult), reads=[RV, Rconst], writes=[Racc])
            for tap in (1, 2):
                op("dve", lambda h, tap=tap: h.scalar_tensor_tensor(
                    out=acc[:], in0=Vbuf[:, b0 + tap:b0 + tap + 512], scalar=CONVW[:, mi, j, tap:tap + 1], in1=acc[:],
                    op0=ALU.mult, op1=ALU.add), reads=[RV, Racc, Rconst], writes=[Racc])
            op("dve", lambda h: h.tensor_tensor(out=yc[:, tsl], in0=acc[:], in1=PSB[2][:], op=ALU.mult),
               reads=[Racc, RPS[2]], writes=[Ryc])

        for j in range(4):
            for kk in range(0, 8, 4):
                dma("pool", lambda h, j=j, kk=kk: h.dma_start(out=convw[:, kk:kk + 4, :],
                                                              in_=dr["hy_conv_in"][mi, j, :, kk:kk + 4, :]),
                    reads=[RT1, RT2], writes=[Rconvw])
            for tb in range(4):
                stage_a(j, tb)
            outproj_chunk(l, wout, Rwout, j, yc, Ryc)
        P.barrier()

        qT = arena(HT_flat, 0, [128, 2048], BF16)
        kT = arena(HT_flat, 4096, [128, 2048], BF16)
        Vaug = arena(HT_flat, 8192, [128, 16, 128], BF16)
        PTs = [arena(HT_flat, 12288 + i * 1024, [128, 512], BF16) for i in range(3)]
        LNS = arena(HT_flat, 15360, [128, 512], F32)
        RCP = arena(HT_flat, 17408, [128, 512], F32)
        yh = arena(HT_flat, 19456, [128, 2048], BF16)
        U1 = arena(HT_flat, 23552, [128, 512], F32)
        U2 = arena(HT_flat, 25600, [128, 512], F32)
        RqT, RkT, RVa, RLNS, RRCP, Ryh, RU1, RU2 = [Res(n) for n in ("qT", "kT", "Vaug", "LNS", "RCP", "yh", "U1", "U2")]
        pt_rot = Rot([(PTs[i], Res(f"pt{i}")) for i in range(3)])
        op("dve", lambda h: h.memset(Vaug[:, :, 64:128], 1.0), writes=[RVa])
        op("dve", lambda h: h.memset(qT[64:128, :], 0.0), writes=[RqT])
        op("dve", lambda h: h.memset(kT[64:128, :], 0.0), writes=[RkT])
        scale = 96.0 ** -0.5

        def head(hh):
            def qproj(tb):
                tsl = slice(tb * 512, (tb + 1) * 512)
                for ab, bank in ((0, 5), (1, 6)):
                    for c in range(2):
                        c0 = (hh * 2 + ab) * 96
                        op("pe", lambda h, c=c, c0=c0, bank=bank: h.matmul(
                            PSB[bank][0:96, :], lhsT=wuq[:, c, c0:c0 + 96], rhs=cqn[:, c, tsl], start=(c == 0), stop=(c == 1)),
                           reads=[Rwuq, Rcqn], writes=[RPS[bank]])
                op("dve", lambda h: h.tensor_tensor(out=U1[0:96, :], in0=PSB[5][0:96, :], in1=CC[0:96, tsl], op=ALU.mult),
                   reads=[RPS[5], Rtab], writes=[RU1])
                op("dve", lambda h: h.tensor_tensor(out=U2[0:96, :], in0=PSB[6][0:96, :], in1=SS[0:96, tsl], op=ALU.mult),
                   reads=[RPS[6], Rtab], writes=[RU2])
                op("dve", lambda h: h.tensor_tensor(out=qT[0:96, tsl], in0=U1[0:96, :], in1=U2[0:96, :], op=ALU.add),
                   reads=[RU1, RU2], writes=[RqT])

            def kproj(tb):
                tsl = slice(tb * 512, (tb + 1) * 512)
                op("pe", lambda h: h.matmul(PSB[5][0:64, :], lhsT=wukv[:, hh * 128:hh * 128 + 64], rhs=ckvn[:, tsl],
                                            start=True, stop=True), reads=[Rwukv, Rckvn], writes=[RPS[5]])
                op("act", lambda h: h.activation(out=kT[0:64, tsl], in_=PSB[5][0:64, :], func=AF.Copy),
                   reads=[RPS[5]], writes=[RkT])

            def vproj(half):
                for k8 in range(8):
                    kt = half * 8 + k8
                    op("pe", lambda h, k8=k8, kt=kt: h.matmul(
                        PSB[6][:, k8 * 64:(k8 + 1) * 64], lhsT=ckvn[:, kt * 128:(kt + 1) * 128],
                        rhs=wukv[:, hh * 128 + 64:hh * 128 + 128], start=True, stop=True),
                       reads=[Rwukv, Rckvn], writes=[RPS[6]])
                op("act", lambda h: h.activation(out=Vaug[:, half * 8:(half + 1) * 8, 0:64],
                                                 in_=PSB[6][:].rearrange("p (a b) -> p a b", b=64), func=AF.Copy),
                   reads=[RPS[6]], writes=[RVa])

            for tb in range(4):
                qproj(tb)
            for tb in range(4):
                kproj(tb)
            op("dve", lambda h: h.tensor_copy(out=kT[64:96, :], in_=KR[64:96, :]), reads=[RKR], writes=[RkT])
            for half in range(2):
                vproj(half)

            def qblock(qb):
                qsl = slice(qb * 512, (qb + 1) * 512)
                tiles = causal_tiles(qb)
                ob = o_rot.next()
                attn_block(qT[:, qsl], lambda kt: kT[:, kt * 128:(kt + 1) * 128], lambda kt: Vaug[:, kt, :],
                           tiles, scale, ob, RqT, RkT, RVa, pt_rot)
                d0 = (hh % 2) * 64

                def stage0():
                    op("act", lambda h: h.activation(out=LNS[0:64, :], in_=PSB[ob][64:128, :], func=AF.Ln),
                       reads=[RPS[ob]], writes=[RLNS])

                def stage1():
                    op("act", lambda h: h.activation(out=RCP[0:64, :], in_=LNS[0:64, :], func=AF.Exp, scale=-1.0),
                       reads=[RLNS], writes=[RRCP])
                    op("dve", lambda h: h.tensor_tensor(out=yh[d0:d0 + 64, qsl], in0=PSB[ob][0:64, :], in1=RCP[0:64, :],
                                                        op=ALU.mult), reads=[RPS[ob], RRCP], writes=[Ryh])
                PEND.append((0, stage0))
                PEND.append((1, stage1))
            for qb in range(4):
                qblock(qb)
            flush_pending()
            if hh % 2 == 1:
                outproj_chunk(l, wout, Rwout, 4 + hh // 2, yh, Ryh)

        for hh in range(8):
            head(hh)

    def nsa(l, mi):
        stw = arena(YT, 0, [128, 8, 560], BF16)
        vw = arena(YT, 8960, [128, 8, 256], BF16)
        klo = arena(YT, 13056, [128, 2048], BF16)
        khi = arena(YT, 17152, [128, 2048], BF16)
        hidT = arena(YT, 21248, [128, 128], BF16)
        w1 = arena(WA, 0, [128, 2, 32, 128], BF16)
        KSa = [arena(WA, 16384, [128, 2048], BF16), arena(WA, 45056, [128, 2048], BF16)]
        KW = arena(WA, 20480, [128, 2048], BF16)
        VA = [[arena(WA, 24576 + (br * 2 + g) * 4096, [128, 16, 128], BF16) for g in range(2)] for br in range(2)]
        cmask = arena(WA, 40960, [128, 2048], BF16)
        LNG = arena(YT, 25600 - 4096, [128, 2048], BF16)
        kTc = arena(YT, 31744, [128, 128], BF16)
        VcA = [arena(YT, 32000 + g * 256, [128, 128], BF16) for g in range(2)]
        agg = arena(YT, 32512, [128, 64], BF16)
        tf = TMP[:].rearrange("p a b -> p (a b)")
        imp2 = tf[:, 0:32]
        selm = tf[:, 32:64]
        top8 = tf[:, 64:72]
        rc1 = tf[:, 72:73]
        gateb = tf[:, 73:74]
        petab = tf[:, 80:144].rearrange("p (a b) -> p a b", b=32)
        w2t = tf[:, 144:240].bitcast(BF16)
        negm = tf[:, 240:256].bitcast(BF16)
        bsel = bview(RS)[:, :].rearrange("p (a b) -> p a b", b=128)
        (Rstw, Rvw, Rklo, Rkhi, Rhid, Rw1, RKS, RKW, RVA, Rcm, RLNG, RkTc, RVc, Ragg, Rsm, Rw2, Rbsel, Rqw) = [
            Res(n) for n in ("stw", "vw", "klo", "khi", "hid", "w1", "KS", "KW", "VA", "cm", "LNG", "kTc", "Vc",
                             "agg", "sm", "w2", "bsel", "qw")]
        for kk in range(0, 8, 4):
            dma("pool", lambda h, kk=kk: h.dma_start(out=stw[:, kk:kk + 4, :], in_=dr["nsa_st_in"][mi, :, kk:kk + 4, :]),
                writes=[Rstw])
        dma("pool", lambda h: h.dma_start(out=vw[:], in_=dr["nsa_v_in"][mi]), writes=[Rvw])
        for kv in range(2):
            dma("pool", lambda h, kv=kv: h.dma_start(out=w1[:, kv], in_=dr["nsa_w1"][mi, :, kv]), writes=[Rw1])
        dma("pool", lambda h: h.dma_start(out=w2t, in_=dr["nsa_w2"][mi]), writes=[Rw2])
        dma("sp", lambda h: h.dma_start(out=petab, in_=dr["nsa_pe"][mi]), writes=[Rsm])
        dma("sp", lambda h: h.dma_start(out=gateb[0:48, :], in_=dr["nsa_gate_b"][mi]), writes=[Rsm])
        dma("sp", lambda h: h.dma_start(out=cmask, in_=dr["cmask"]), writes=[Rcm])
        dma("sp", lambda h: h.dma_start(out=agg[:, 0:33], in_=dr["agg"]), writes=[Ragg])
        dma("sp", lambda h: h.dma_start(out=KSa[0][64:96, :], in_=dr["onehot"]), writes=[Rbsel])
        dma("sp", lambda h: h.dma_start(out=KSa[1][0:32, :], in_=dr["onehot"]), writes=[Rbsel])
        op("dve", lambda h: h.memset(KSa[1][32:64, :], 0.0), writes=[RKS])
        for br in range(2):
            for g in range(2):
                op("dve", lambda h, br=br, g=g: h.memset(VA[br][g][:, :, 64:128], 1.0), writes=[RVA])
        for g in range(2):
            op("dve", lambda h, g=g: h.memset(VcA[g][:, 0:64], 0.0), writes=[RVc])
            op("dve", lambda h, g=g: h.memset(VcA[g][:, 64:128], 1.0), writes=[RVc])
        op("dve", lambda h: h.memset(kTc[:], 0.0), writes=[RkTc])

        def proj(c0, c1, bank, tb, mrows=128):
            tsl = slice(tb * 512, (tb + 1) * 512)
            for k in range(8):
                op("pe", lambda h, k=k: h.matmul(PSB[bank][0:mrows, :], lhsT=stw[:, k, c0:c1], rhs=HT[:, k, tsl],
                                                 start=(k == 0), stop=(k == 7)), reads=[Rstw, RH[tb]], writes=[RPS[bank]])

        def setup_tb(tb):
            tsl = slice(tb * 512, (tb + 1) * 512)
            proj(256, 384, 0, tb)
            op("act", lambda h: h.activation(out=KSa[0][0:64, tsl], in_=PSB[0][0:64, :], func=AF.Copy), reads=[RPS[0]], writes=[RKS])
            op("act", lambda h: h.activation(out=KSa[1][64:128, tsl], in_=PSB[0][64:128, :], func=AF.Copy), reads=[RPS[0]], writes=[RKS])
            proj(384, 512, 1, tb)
            op("act", lambda h: h.activation(out=KW[:, tsl], in_=PSB[1][:], func=AF.Copy), reads=[RPS[1]], writes=[RKW])
            proj(512, 560, 2, tb, mrows=48)
            op("dve", lambda h: h.tensor_scalar(out=T1s[0:48, :], in0=PSB[2][0:48, :], scalar1=gateb[0:48, :], scalar2=None,
                                                op0=ALU.add), reads=[RPS[2], Rsm], writes=[RT1s])
            op("act", lambda h: h.activation(out=T1s[0:48, :], in_=T1s[0:48, :], func=AF.Exp, scale=-1.0),
               reads=[RT1s], writes=[RT1s])
            op("act", lambda h: h.activation(out=T1s[0:48, :], in_=T1s[0:48, :], func=AF.Ln, bias=FC[0:48, 3:4]),
               reads=[RT1s, Rconst], writes=[RT1s])
            op("dve", lambda h: h.tensor_scalar(out=LNG[0:48, tsl], in0=T1s[0:48, :], scalar1=-1.0, scalar2=None,
                                                op0=ALU.mult), reads=[RT1s], writes=[RLNG])
        T1s = arena(YT, 25600, [128, 512], F32)
        RT1s = Res("T1s")
        for tb in range(4):
            setup_tb(tb)

        def vproj(kt):
            for k in range(8):
                op("pe", lambda h, k=k: h.matmul(PSB[3][:, 0:256], lhsT=HT[:, k, kt * 128:(kt + 1) * 128], rhs=vw[:, k, :],
                                                 start=(k == 0), stop=(k == 7)), reads=[Rvw, RH[kt // 4]], writes=[RPS[3]])
            for br in range(2):
                for g in range(2):
                    c0 = (br * 2 + g) * 64
                    op("act", lambda h, br=br, g=g, c0=c0: h.activation(out=VA[br][g][:, kt, 0:64], in_=PSB[3][:, c0:c0 + 64],
                                                                        func=AF.Copy), reads=[RPS[3]], writes=[RVA])
        for kt in range(16):
            vproj(kt)

        def compress(kv):
            def addpe(tb):
                tsl = slice(tb * 512, (tb + 1) * 512)
                proj(kv * 128, (kv + 1) * 128, 4, tb)
                for (dst, rdst, lo) in ((klo, Rklo, 0), (khi, Rkhi, 16)):
                    op("dve", lambda h, dst=dst, lo=lo: h.tensor_tensor(
                        out=dst[:, tsl].rearrange("p (n l) -> p n l", l=16),
                        in0=PSB[4][:].rearrange("p (n l) -> p n l", l=16),
                        in1=petab[:, kv, lo:lo + 16].unsqueeze(1).broadcast_to([128, 32, 16]), op=ALU.add),
                       reads=[RPS[4], Rsm], writes=[rdst])
            for tb in range(4):
                addpe(tb)

            def cgroup(g):
                r = slice(64 * g, 64 * g + 64)
                for ll in range(32):
                    src = klo if ll < 16 else khi
                    rsrc = Rklo if ll < 16 else Rkhi
                    op("pe", lambda h, ll=ll, src=src: h.matmul(
                        PSB[5][:, 0:127], lhsT=w1[r, kv, ll, :], rhs=src[r, ll:ll + 16 * 126 + 1:16],
                        start=(ll == 0), stop=(ll == 31)), reads=[Rw1, rsrc], writes=[RPS[5]])
                op("act", lambda h: h.activation(out=hidT[:, 0:127], in_=PSB[5][:, 0:127], func=AF.Silu),
                   reads=[RPS[5]], writes=[Rhid])
                if kv == 0:
                    op("pe", lambda h: h.matmul(PSB[6][:, 0:127], lhsT=w2t[:, 0:128], rhs=hidT[:, 0:127], start=True, stop=True),
                       reads=[Rw2, Rhid], writes=[RPS[6]])
                    op("act", lambda h: h.activation(out=kTc[r, 0:127], in_=PSB[6][r, 0:127], func=AF.Copy),
                       reads=[RPS[6]], writes=[RkTc])
                else:
                    op("pe", lambda h: h.matmul(PSB[6][0:127, 0:64], lhsT=hidT[:, 0:127], rhs=w2t[:, 128:192], start=True, stop=True),
                       reads=[Rw2, Rhid], writes=[RPS[6]])
                    op("act", lambda h: h.activation(out=VcA[g][0:127, 0:64], in_=PSB[6][0:127, 0:64], func=AF.Copy),
                       reads=[RPS[6]], writes=[RVc])
            for g in range(2):
                cgroup(g)
        compress(0)
        if DBG:
            dma("sp", lambda h: h.dma_start(out=DBG["klo"], in_=klo), reads=[Rklo], track=Rdbg)
        compress(1)
        if DBG:
            dma("sp", lambda h: h.dma_start(out=DBG["kTc"], in_=kTc), reads=[RkTc], track=Rdbg)
            for g in range(2):
                dma("sp", lambda h, g=g: h.dma_start(out=DBG["Vc"][g], in_=VcA[g]), reads=[RVc], track=Rdbg)
            dma("sp", lambda h: h.dma_start(out=DBG["LNG"], in_=LNG[0:48, :]), reads=[RLNG], track=Rdbg)
        P.barrier()

        qw = arena(WA, 0, [128, 8, 1024], BF16)
        qTh = arena(YT, 0, [128, 2048], BF16)
        PTs = [arena(YT, 4096 + i * 1024, [128, 512], BF16) for i in range(3)]
        LFs = [arena(YT, 7168, [128, 512], F32), arena(YT, 9216, [128, 512], F32)]
        lf_rot = Rot([(LFs[i], Res(f"lf{i}")) for i in range(2)])
        T1 = arena(YT, 11264, [128, 512], F32)
        OACC = arena(YT, 13312, [128, 512], F32)
        selA = arena(YT, 7168, [128, 16, 32], F32)
        selB = arena(YT, 9216, [128, 16, 32], F32)
        yh = arena(YT, 15360, [128, 2048], BF16)
        woc = arena(YT, 19456, [128, 1, 1024], BF16)
        imp = arena(YT, 25600, [128, 16, 32], F32)
        gsel = arena(YT, 25600, [128, 48, 64], BF16)
        RqT, RT1, ROACC, Rsel, Ryh, Rwoc, RnegT, Rimp, Rgsel = [Res(n) for n in (
            "qTh", "T1", "OACC", "selAB", "yh", "woc", "negT", "imp", "gsel")]
        pt_rot = Rot([(PTs[i], Res(f"npt{i}")) for i in range(3)])
        for kk in range(0, 8, 2):
            dma("pool", lambda h, kk=kk: h.dma_start(out=qw[:, kk:kk + 2, :], in_=dr["nsa_q_in"][mi, :, kk:kk + 2, :]),
                writes=[Rqw])
        scale = 0.125

        qp_rot = Rot([5, 6])
        rc_rot = Rot([(tf[:, 256 + 4 * i:260 + 4 * i], Res(f"rc4{i}")) for i in range(2)])
        tm_rot = Rot([(tf[:, 272 + 128 * i:400 + 128 * i].rearrange("p (a b) -> p a b", b=32), Res(f"tm4{i}")) for i in range(2)])

        def qproj(hh):
            g = hh // 8
            c0 = hh * 64 - 64 * g
            for tb in range(4):
                def one(tb=tb):
                    tsl = slice(tb * 512, (tb + 1) * 512)
                    m = 64 + 64 * g
                    qb_ = qp_rot.next()
                    for k in range(8):
                        op("pe", lambda h, k=k: h.matmul(PSB[qb_][0:m, :], lhsT=qw[:, k, c0:c0 + m], rhs=HT[:, k, tsl],
                                                         start=(k == 0), stop=(k == 7)), reads=[Rqw, RH[tb]], writes=[RPS[qb_]])
                    r = slice(64 * g, 64 * g + 64)
                    op("dve", lambda h: h.tensor_copy(out=qTh[r, tsl], in_=PSB[qb_][r, :]),
                       reads=[RPS[qb_]], writes=[RqT])
                one()

        def group(g):
            r = slice(64 * g, 64 * g + 64)
            nr = slice(64, 96) if g == 0 else slice(0, 32)
            kr = slice(0, 96) if g == 0 else slice(0, 128)
            dma("sp", lambda h: h.dma_start(out=selA, in_=dr["selA"]), writes=[Rsel])
            dma("sp", lambda h: h.dma_start(out=selB, in_=dr["selB"]), writes=[Rsel])
            op("dve", lambda h: h.memset(imp, 0.0), writes=[Rimp])

            def p1(hh, qb):
                qsl = slice(qb * 512, (qb + 1) * 512)
                sbk = s_rot.next()
                op("pe", lambda h: h.matmul(PSB[sbk][:], lhsT=kTc[r, :], rhs=qTh[r, qsl], start=True, stop=False),
                   reads=[RkTc, RqT], writes=[RPS[sbk]])
                op("pe", lambda h: h.matmul(PSB[sbk][:], lhsT=ident[:], rhs=cmask[:, qsl], start=False, stop=True),
                   reads=[Rcm, Rconst], writes=[RPS[sbk]])
                pt, rpt = pt_rot.next()
                op("act", lambda h: h.activation(out=pt, in_=PSB[sbk][:], func=AF.Exp, scale=scale),
                   reads=[RPS[sbk]], writes=[rpt])
                ib = o_rot.next()
                for qi in range(4):
                    op("pe", lambda h, qi=qi: h.matmul(PSB[ib][:, qi * 64:qi * 64 + 33], lhsT=pt[:, qi * 128:(qi + 1) * 128],
                                                       rhs=agg[:, 0:33], start=True, stop=True), reads=[rpt, Ragg], writes=[RPS[ib]])
                v4 = PSB[ib][:, 0:256].rearrange("p (a b) -> p a b", b=64)
                rc4, rrc4 = rc_rot.next()
                tm4, rtm4 = tm_rot.next()
                op("dve", lambda h: h.tensor_scalar(out=rc4.unsqueeze(2), in0=v4[:, :, 32:33], scalar1=1e-20, scalar2=None,
                                                    op0=ALU.max), reads=[RPS[ib]], writes=[rrc4])
                op("dve", lambda h: h.reciprocal(out=rc4, in_=rc4), reads=[rrc4], writes=[rrc4])
                op("dve", lambda h: h.tensor_tensor(out=tm4, in0=v4[:, :, 0:32], in1=rc4.unsqueeze(2).broadcast_to([128, 4, 32]),
                                                    op=ALU.mult), reads=[RPS[ib], rrc4], writes=[rtm4])
                op("dve", lambda h: h.tensor_tensor(out=imp[:, qb * 4:(qb + 1) * 4, :], in0=imp[:, qb * 4:(qb + 1) * 4, :], in1=tm4,
                                                    op=ALU.add), reads=[rtm4, Rimp], writes=[Rimp])
            for hh in range(8 * g, 8 * g + 8):
                qproj(hh)
                for qb in range(4):
                    p1(hh, qb)

            def selq(qb):
                for qi in range(4):
                    def sub(qi=qi):
                        qt = qb * 4 + qi
                        op("dve", lambda h: h.tensor_tensor(out=imp2, in0=imp[:, qt, :], in1=selA[:, qt, :], op=ALU.mult),
                           reads=[Rimp, Rsel], writes=[Rsm])
                        op("dve", lambda h: h.tensor_tensor(out=imp2, in0=imp2, in1=selB[:, qt, :], op=ALU.add),
                           reads=[Rsm, Rsel], writes=[Rsm])
                        op("dve", lambda h: h.max(out=top8, in_=imp2), reads=[Rsm], writes=[Rsm])
                        op("dve", lambda h: h.tensor_scalar(out=selm, in0=imp2, scalar1=top8[:, 7:8], scalar2=None,
                                                            op0=ALU.is_ge), reads=[Rsm], writes=[Rsm])
                        op("dve", lambda h: h.tensor_scalar(out=selm, in0=selm, scalar1=-1.0, scalar2=None, op0=ALU.add),
                           reads=[Rsm], writes=[Rsm])
                        op("dve", lambda h: h.tensor_scalar(out=negm, in0=selm, scalar1=-NEG, scalar2=None, op0=ALU.mult),
                           reads=[Rsm], writes=[Rsm])
                        op("pe", lambda h: h.matmul(PSB[6][0:32, qi * 128:(qi + 1) * 128], lhsT=negm, rhs=ident[:],
                                                    start=True, stop=True), reads=[Rsm, Rconst], writes=[RPS[6]])
                    sub()
                op("act", lambda h: h.activation(out=qTh[nr, qb * 512:(qb + 1) * 512], in_=PSB[6][0:32, :], func=AF.Copy),
                   reads=[RPS[6]], writes=[RnegT])
            for qb in range(4):
                selq(qb)
            if DBG:
                dma("sp", lambda h: h.dma_start(out=DBG["negT"][g], in_=qTh[nr, :]), reads=[RnegT], track=Rdbg)
                dma("sp", lambda h: h.dma_start(out=DBG["imp"][g], in_=imp.rearrange("p a b -> p (a b)")), reads=[Rimp], track=Rdbg)
                P.wait_tokens("sp", [("dma", Rdbg, Rdbg.cnt)])
            P.barrier()
            dma("sp", lambda h: h.dma_start(out=gsel[0:48], in_=dr["gsel"]), writes=[Rgsel])

            def branch(hh, qb, br, first):
                qsl = slice(qb * 512, (qb + 1) * 512)
                ob = o_rot.next()
                q_ap = qTh[r, qsl]
                if br == 0:
                    tiles = [(0, cmask[:, qsl], 0, 512)]
                    kf = lambda kt: kTc[r, :]
                    vf = lambda kt: VcA[g][:]
                    rk, rv = RkTc, RVc
                elif br == 1:
                    tiles = causal_tiles(qb)
                    kf = lambda kt: KSa[g][kr, kt * 128:(kt + 1) * 128]
                    vf = lambda kt: VA[0][g][:, kt, :]
                    rk, rv = RKS, RVA
                    q_ap = qTh[kr, qsl]
                else:
                    tiles = window_tiles(qb)
                    kf = lambda kt: KW[r, kt * 128:(kt + 1) * 128]
                    vf = lambda kt: VA[1][g][:, kt, :]
                    rk, rv = RKW, RVA
                attn_block(q_ap, kf, vf, tiles, scale, ob, RqT, rk, rv, pt_rot, extra_reads=[RnegT, Rbsel, Rcm])
                row = hh * 3 + br
                FAC, RFAC = lf_rot.next()
                last = (br == NSA_BR[-1])
                d0 = (hh % 2) * 64

                def stage0():
                    op("act", lambda h: h.activation(out=FAC[0:64, :], in_=PSB[ob][64:128, :], func=AF.Ln, bias=FC[0:64, 4:5]),
                       reads=[RPS[ob], Rconst], writes=[RFAC])
                    op("pe", lambda h: h.matmul(PSB[7][0:64, :], lhsT=gsel[0:48, row, :], rhs=LNG[0:48, qsl], start=True, stop=True),
                       reads=[Rgsel, RLNG], writes=[RPS[7]])
                    op("dve", lambda h: h.tensor_tensor(out=FAC[0:64, :], in0=PSB[7][0:64, :], in1=FAC[0:64, :], op=ALU.subtract),
                       reads=[RPS[7], RFAC], writes=[RFAC])

                def stage1():
                    op("act", lambda h: h.activation(out=FAC[0:64, :], in_=FAC[0:64, :], func=AF.Exp), reads=[RFAC], writes=[RFAC])
                    if first and last:
                        op("dve", lambda h: h.tensor_tensor(out=yh[d0:d0 + 64, qsl], in0=PSB[ob][0:64, :], in1=FAC[0:64, :],
                                                            op=ALU.mult), reads=[RPS[ob], RFAC], writes=[Ryh])
                    elif first:
                        op("dve", lambda h: h.tensor_tensor(out=OACC[0:64, :], in0=PSB[ob][0:64, :], in1=FAC[0:64, :], op=ALU.mult),
                           reads=[RPS[ob], RFAC], writes=[ROACC])
                    else:
                        op("dve", lambda h: h.tensor_tensor(out=T1[0:64, :], in0=PSB[ob][0:64, :], in1=FAC[0:64, :], op=ALU.mult),
                           reads=[RPS[ob], RFAC], writes=[RT1])
                        if not last:
                            op("dve", lambda h: h.tensor_tensor(out=OACC[0:64, :], in0=OACC[0:64, :], in1=T1[0:64, :], op=ALU.add),
                               reads=[ROACC, RT1], writes=[ROACC])
                        else:
                            op("dve", lambda h: h.tensor_tensor(out=yh[d0:d0 + 64, qsl], in0=OACC[0:64, :], in1=T1[0:64, :],
                                                                op=ALU.add), reads=[ROACC, RT1], writes=[Ryh])
                PEND.append((0, stage0))
                PEND.append((1, stage1))

            for hh in range(8 * g, 8 * g + 8):
                qproj(hh)
                for qb in range(4):
                    for br in NSA_BR:
                        branch(hh, qb, br, br == NSA_BR[0])
                flush_pending()
                if hh % 2 == 1:
                    kc = hh // 2
                    dma("pool", lambda h, kc=kc: h.dma_start(out=woc[:, 0, :], in_=dr["nsa_w_out"][mi, :, kc, :]),
                        writes=[Rwoc])
                    outproj_chunk(l, woc, Rwoc, 0, yh, Ryh)
            P.barrier()

        for g in range(2):
            group(g)

    def mixer(l, kind, mi):
        if kind == 0:
            mla(l, mi)
        else:
            nsa(l, mi)

    for (l, kind, mi) in lmap:
        for i in range(3):
            if i == 1:
                if do_mixer:
                    mod_norm(l, 1)
                    P.barrier()
                    mixer(l, kind, mi)
                    P.barrier()
                continue
            if not do_ffn:
                continue
            pre = {0: ffn_load(l, i, 0)}
            mod_norm(l, i)
            ffn(l, i, pre)
            P.barrier()

    Rout = Res("out")
    OB = [arena(WA, s * 16384, [128, 8, 512], F32) for s in range(2)]
    ROB = [Res("ob0"), Res("ob1")]
    def fin(tb):
        tsl = slice(tb * 512, (tb + 1) * 512)
        s = tb % 2
        if do_final:
            rs, rrs = norm_stats(tb)
            for k in range(8):
                op("dve", lambda h, k=k, s=s: h.scalar_tensor_tensor(
                    out=OB[s][:, k, :], in0=X[:, k, tsl], scalar=FG[:, k:k + 1], in1=rs, op0=ALU.mult, op1=ALU.mult),
                   reads=[RX[k][tb], rrs, Rconst], writes=[ROB[s]])
            op("dve", lambda h, s=s: h.tensor_scalar(out=OB[s][:], in0=OB[s][:], scalar1=32.0, scalar2=None, op0=ALU.mult),
               reads=[ROB[s]], writes=[ROB[s]])
        else:
            op("dve", lambda h, s=s: h.tensor_copy(out=OB[s][:], in_=X[:, :, tsl]),
               reads=[RX[k][tb] for k in range(8)], writes=[ROB[s]])
        dma("sp", lambda h, s=s: h.dma_start(out=outT[:, :, tsl], in_=OB[s][:]), reads=[ROB[s]], track=Rout)
    for tb in range(4):
        fin(tb)
    P.wait_tokens("sp", [("dma", Rout, Rout.cnt)])
    P.run_block()
    P.close()
    return nc


LAST_RESULTS = None
_DT = {np.dtype(np.float32): F32, np.dtype(np.int32): I32, np.dtype(ml_dtypes.bfloat16): BF16}


def run(inp, layers=(0, 1, 2, 3), **kw):
    layers = list(layers)
    inp = dict(inp)
    lmap = []
    if len(layers) < DEPTH:
        ev = sorted({l // 2 for l in layers if l % 2 == 0}) or [0]
        od = sorted({l // 2 for l in layers if l % 2 == 1}) or [0]
        ls = layers or [0]
        for k in ("ada_w", "ada_b", "norm_g", "ff_w13", "ff_w2"):
            inp[k] = np.asarray(inp[k])[ls]
        for k in list(inp):
            if k.startswith("hy_"):
                inp[k] = np.asarray(inp[k])[ev]
            if k.startswith("nsa_"):
                inp[k] = np.asarray(inp[k])[od]
        for n, l in enumerate(layers):
            lmap.append((n, l % 2, (ev if l % 2 == 0 else od).index(l // 2)))
    else:
        lmap = [(l, l % 2, l // 2) for l in layers]
    sh = host_prep(inp)
    cores = [host_core(inp, b) for b in range(8)]
    shapes = {k: (v.shape, _DT[v.dtype]) for k, v in {**sh, **cores[0]}.items()}
    nc = build_program(shapes, lmap=tuple(lmap), **kw)
    in_maps = [{**sh, **cores[b]} for b in range(8)]
    res = run_bass_kernel_spmd(nc, in_maps, core_ids=list(range(8)))
    global LAST_RESULTS
    LAST_RESULTS = res.results
    out = np.stack([np.asarray(r["outT"], np.float32) for r in res.results])
    return _c(out.transpose(0, 3, 2, 1).reshape(8, S, D))


def kernel(**inputs):
    return run(inputs)
```
